# Optimizing a Trainium2 kernel written in Bass

```python
import math
import jax, jax.numpy as jnp
from jax import lax
import numpy as np

D_MODEL = 1024
BATCH = 2
SEQ = 8192
DEPTH = 2

N_A_LAYERS = DEPTH // 2
N_B_LAYERS = DEPTH - N_A_LAYERS

CONV_WIDTH = D_MODEL
CONV_K = 3

HEAD_DIM = 64
N_Q_HEADS = D_MODEL // HEAD_DIM
N_KV_HEADS = max(1, N_Q_HEADS // 8)
GROUP = N_Q_HEADS // N_KV_HEADS
ATTN_WIDTH = N_Q_HEADS * HEAD_DIM
KV_WIDTH = N_KV_HEADS * HEAD_DIM
WINDOW = 128
BLOCK = 128

N_BUCKETS = 32
MAX_DISTANCE = 128

EPS = 1e-6
NEG_INF = -1e30

kernel_name = "yoco_shortconv_swa_sink_hybrid"


def rmsnorm(x, g):
    xf = x.astype(jnp.float32)
    y = xf * lax.rsqrt(jnp.mean(xf * xf, axis=-1, keepdims=True) + EPS) * g.astype(jnp.float32)
    return y.astype(x.dtype)


def t5_causal_bucket(dist):
    max_exact = N_BUCKETS // 2
    is_small = dist < max_exact
    d = jnp.maximum(dist, 1).astype(jnp.float32)
    large = max_exact + (jnp.log(d / max_exact) / math.log(MAX_DISTANCE / max_exact)
                         * (N_BUCKETS - max_exact)).astype(jnp.int32)
    large = jnp.minimum(large, N_BUCKETS - 1)
    return jnp.where(is_small, dist, large)


def short_conv_mixer(h, w_in, conv_w, w_out):
    proj = h @ w_in
    b_gate, c_gate, u, z = jnp.split(proj, 4, axis=-1)
    v = c_gate * u
    conv = lax.conv_general_dilated(
        v, conv_w[:, None, :].astype(v.dtype),
        window_strides=(1,), padding=[(CONV_K - 1, 0)],
        dimension_numbers=("NWC", "WIO", "NWC"),
        feature_group_count=CONV_WIDTH)
    y = b_gate * conv * jax.nn.silu(z)
    return y @ w_out


def shared_kv(h, kv_norm, w_kv):
    bsz, seq, _ = h.shape
    nb = seq // BLOCK
    kv = rmsnorm(h, kv_norm) @ w_kv
    k, v = jnp.split(kv, 2, axis=-1)
    k = k.reshape(bsz, nb, BLOCK, N_KV_HEADS, HEAD_DIM)
    v = v.reshape(bsz, nb, BLOCK, N_KV_HEADS, HEAD_DIM)

    def band(t):
        prev = jnp.concatenate([jnp.zeros_like(t[:, :1]), t[:, :-1]], axis=1)
        return jnp.concatenate([prev, t], axis=2)

    return band(k), band(v)


def banded_bias_and_mask(nb, rel_bias):
    q_loc = jnp.arange(BLOCK, dtype=jnp.int32)[:, None]
    s_loc = jnp.arange(2 * BLOCK, dtype=jnp.int32)[None, :]
    dist = q_loc + BLOCK - s_loc
    in_window = (dist >= 0) & (dist < WINDOW)
    bucket = t5_causal_bucket(jnp.maximum(dist, 0))
    bias = rel_bias.astype(jnp.float32)[bucket]
    bias = jnp.transpose(bias, (2, 0, 1)).reshape(N_KV_HEADS, GROUP, BLOCK, 2 * BLOCK)
    blk = jnp.arange(nb, dtype=jnp.int32)[:, None, None]
    exists = (blk > 0) | (s_loc >= BLOCK)[None]
    mask = in_window[None] & exists
    return bias, mask


def swa_sink_attention(q, keys, vals, sinks, bias, mask):
    bsz, seq, _ = q.shape
    nb = seq // BLOCK
    qb = q.reshape(bsz, nb, BLOCK, N_KV_HEADS, GROUP, HEAD_DIM)
    scores = jnp.einsum("bnqkgd,bnskd->bnkgqs", qb, keys).astype(jnp.float32)
    logits = scores * (HEAD_DIM ** -0.5) + bias[None, None]
    logits = jnp.where(mask[None, :, None, None], logits, NEG_INF)
    sink = sinks.astype(jnp.float32).reshape(1, 1, N_KV_HEADS, GROUP, 1, 1)
    m = jnp.maximum(jnp.max(logits, axis=-1, keepdims=True), sink)
    p = jnp.exp(logits - m)
    p = p / (jnp.sum(p, axis=-1, keepdims=True) + jnp.exp(sink - m))
    out = jnp.einsum("bnkgqs,bnskd->bnqkgd", p.astype(vals.dtype), vals)
    return out.reshape(bsz, seq, ATTN_WIDTH)


def setup_inputs(seed: int = 0) -> dict:
    key = jax.random.key(seed)
    ks = jax.random.split(key, 16)
    f32 = jnp.float32
    nrm = lambda k, shape, s: jax.random.normal(k, shape, f32) * s
    return {
        "x": nrm(ks[0], (BATCH, SEQ, D_MODEL), 1.0),
        "a_pre_norm": 1.0 + nrm(ks[1], (N_A_LAYERS, D_MODEL), 0.05),
        "a_w_in": nrm(ks[2], (N_A_LAYERS, D_MODEL, 4 * CONV_WIDTH), D_MODEL ** -0.5),
        "a_conv_w": nrm(ks[3], (N_A_LAYERS, CONV_K, CONV_WIDTH), CONV_K ** -0.5),
        "a_w_out": nrm(ks[4], (N_A_LAYERS, CONV_WIDTH, D_MODEL), CONV_WIDTH ** -0.5),
        "a_post_norm": 1.0 + nrm(ks[5], (N_A_LAYERS, D_MODEL), 0.05),
        "kv_norm": 1.0 + nrm(ks[6], (D_MODEL,), 0.05),
        "w_kv": nrm(ks[7], (D_MODEL, 2 * KV_WIDTH), D_MODEL ** -0.5),
        "rel_bias": nrm(ks[8], (N_BUCKETS, N_Q_HEADS), 0.1),
        "b_pre_norm": 1.0 + nrm(ks[9], (N_B_LAYERS, D_MODEL), 0.05),
        "b_w_in": nrm(ks[10], (N_B_LAYERS, D_MODEL, 2 * ATTN_WIDTH), D_MODEL ** -0.5),
        "b_sinks": nrm(ks[11], (N_B_LAYERS, N_Q_HEADS), 0.5),
        "b_w_out": nrm(ks[12], (N_B_LAYERS, ATTN_WIDTH, D_MODEL), ATTN_WIDTH ** -0.5),
        "b_post_norm": 1.0 + nrm(ks[13], (N_B_LAYERS, D_MODEL), 0.05),
    }


def reference(x, a_pre_norm, a_w_in, a_conv_w, a_w_out, a_post_norm,
              kv_norm, w_kv, rel_bias,
              b_pre_norm, b_w_in, b_sinks, b_w_out, b_post_norm):
    h = x
    nb = x.shape[1] // BLOCK
    bias, mask = banded_bias_and_mask(nb, rel_bias)
    keys = vals = None
    for layer in range(DEPTH):
        if layer < N_A_LAYERS:
            i = layer
            y = short_conv_mixer(rmsnorm(h, a_pre_norm[i]), a_w_in[i], a_conv_w[i], a_w_out[i])
            h = h + rmsnorm(y, a_post_norm[i])
            if layer == N_A_LAYERS - 1:
                keys, vals = shared_kv(h, kv_norm, w_kv)
        else:
            j = layer - N_A_LAYERS
            qz = rmsnorm(h, b_pre_norm[j]) @ b_w_in[j]
            q, z = jnp.split(qz, 2, axis=-1)
            o = swa_sink_attention(q, keys, vals, b_sinks[j], bias, mask) * jax.nn.silu(z)
            y = o @ b_w_out[j]
            h = h + rmsnorm(y, b_post_norm[j])
    return h
```

```python
import math
from contextlib import ExitStack

import numpy as np
import concourse.bass as bass
import concourse.mybir as mybir
from concourse.bass_utils import run_bass_kernel_spmd

F32 = mybir.dt.float32
BF16 = mybir.dt.bfloat16
AF = mybir.ActivationFunctionType
ALU = mybir.AluOpType

D = 1024
NCORES = 8
TOK_PER_CORE = 2048
NBLK = 16
NB1 = NBLK + 1
NGRP = 4
EPS = 1e-6
MASK_VAL = -80.0
NCOL = 72

COMPUTE = ("pe", "act", "dve", "pool")


class _Op:
    __slots__ = ("eng", "fn", "deps", "signal", "count", "sem", "is_dma", "idx", "rw")

    def __init__(self, eng, fn, is_dma, sem):
        self.eng = eng
        self.fn = fn
        self.deps = []
        self.signal = False
        self.count = None
        self.sem = sem
        self.is_dma = is_dma


class Prog:
    def __init__(self, nc, same_engine_sync=True):
        self.nc = nc
        self.ops = []
        self.res = {}
        self.same_engine_sync = same_engine_sync

    def _r(self, k):
        r = self.res.get(k)
        if r is None:
            r = self.res[k] = [None, []]
        return r

    def op(self, eng, fn, reads=(), writes=(), dma_sem=None, after=()):
        o = _Op(eng, fn, dma_sem is not None, dma_sem)
        o.idx = len(self.ops)
        o.rw = (tuple(reads), tuple(writes))
        deps = {x.idx: x for x in after}
        for k in reads:
            r = self._r(k)
            if r[0] is not None:
                deps[r[0].idx] = r[0]
        for k in writes:
            r = self._r(k)
            if r[0] is not None:
                deps[r[0].idx] = r[0]
            for x in r[1]:
                deps[x.idx] = x
        for k in reads:
            self._r(k)[1].append(o)
        for k in writes:
            r = self._r(k)
            r[0] = o
            r[1] = []
        o.deps = list(deps.values())
        self.ops.append(o)
        return o

    def emit(self, final_waits=()):
        nc = self.nc

        def needs_sync(x, y):
            if x.is_dma or y.is_dma:
                return True
            if x.eng != y.eng:
                return True
            if x.eng == "pe":
                return False
            return self.same_engine_sync

        for y in self.ops:
            y.deps = [x for x in y.deps if needs_sync(x, y)]
            for x in y.deps:
                x.signal = True
        for o in final_waits:
            o.signal = True
        eng_cnt = {e: 0 for e in COMPUTE}
        dma_cnt = {}
        for o in self.ops:
            if not o.signal:
                continue
            if o.is_dma:
                dma_cnt[o.sem] = dma_cnt.get(o.sem, 0) + 16
                o.count = dma_cnt[o.sem]
            else:
                eng_cnt[o.eng] += 1
                o.count = eng_cnt[o.eng]
        sem_names = [e for e in COMPUTE if eng_cnt[e] > 0] + sorted(dma_cnt.keys())
        with ExitStack() as st:
            sems = {n: st.enter_context(nc.semaphore("s_" + n)) for n in sem_names}
            block = st.enter_context(nc.Block())

            def semof(o):
                return sems[o.sem] if o.is_dma else sems[o.eng]

            def run(engname):
                def body(eng):
                    waited = {}
                    for o in self.ops:
                        if o.eng != engname:
                            continue
                        need = {}
                        for x in o.deps:
                            key = x.sem if x.is_dma else x.eng
                            if need.get(key, (0, None))[0] < x.count:
                                need[key] = (x.count, x)
                        for key, (cnt, x) in need.items():
                            if waited.get(key, 0) >= cnt:
                                continue
                            waited[key] = cnt
                            eng.wait_ge(semof(x), cnt)
                        ins = o.fn(eng)
                        if o.signal:
                            ins.then_inc(semof(o), 16 if o.is_dma else 1)
                    if engname == "sp":
                        for o in final_waits:
                            eng.wait_ge(semof(o), o.count)

                return body

            block.tensor(run("pe"))
            block.scalar(run("act"))
            block.vector(run("dve"))
            block.gpsimd(run("pool"))
            block.sync(run("sp"))


def build_nc(debug=False):
    nc = bass.Bass("TRN2", target_bir_lowering=False)

    def din(name, shape, dt=F32):
        return nc.dram_tensor(name, list(shape), dt, kind="ExternalInput").ap()

    xpre_d = din("xpre", [4, D])
    xmain_d = din("xmain", [NB1 * 128, D])
    flag_d = din("flag", [128, 1])
    ident_d = din("ident", [128, 128])
    onesp_d = din("onesp", [128, 256])
    a_w_in_d = din("a_w_in", [D, 4 * D])
    a_w_out_d = din("a_w_out", [D, D])
    b_w_in_d = din("b_w_in", [D, 2 * D])
    b_w_out_d = din("b_w_out", [D, D])
    w_kv_d = din("w_kv", [D, 256])
    g_apre_d = din("g_apre", [128, D])
    g_apost_d = din("g_apost", [128, D])
    g_bpre_d = din("g_bpre", [128, D])
    g_bpost_d = din("g_bpost", [128, D])
    gk_d = din("gk", [128, 8])
    gb_d = din("gb", [128, 8])
    cw_d = din("cw", [128, 24])
    sinks_d = din("sinks", [128, 8])
    bias_d = din("biasT", [128, 4096])
    out_d = nc.dram_tensor("out", [TOK_PER_CORE, D], F32, kind="ExternalOutput").ap()
    dbg_d = None
    if debug:
        dbg_d = nc.dram_tensor("dbg", [NB1 * 128, D], F32, kind="ExternalOutput").ap()

    with ExitStack() as st:
        def sb(name, shape, dt):
            return st.enter_context(nc.sbuf_tensor(name, list(shape), dt))

        h = sb("h", [128, NB1, D], F32)
        arena = sb("arena", [128, 40960], BF16)
        xT = sb("xT", [128, 8, 514], BF16)
        wkp = sb("wkp", [128, 8, 128], BF16)
        kt_nat = sb("kt_nat", [128, 512], BF16)
        wv = sb("wv", [128, 8, 128], BF16)
        g_post = sb("g_post", [128, D], F32)
        otmp = sb("otmp", [128, D], F32)
        xn_t = [sb("xn%d" % i_, [128, D], BF16) for i_ in range(2)]
        xn = xn_t[0]
        g_pre = sb("g_pre", [128, D], BF16)
        interm = sb("interm", [128, 2052], F32)
        yT_u = sb("yT_u", [128, 4096], BF16)
        ktp = sb("ktp", [128, 2, 2, 640], BF16)
        vp = sb("vp", [128, 5, 2, 2, 128], BF16)
        ident = sb("identb", [128, 128], BF16)
        onesp = sb("onespb", [128, 2, 128], BF16)
        ss = sb("ss", [128, NCOL], F32)
        rs = sb("rs", [128, NCOL], F32)
        junk_t = sb("junk", [128, 2], BF16)
        junk = junk_t[:, 0:1].broadcast_to([128, D])
        xn2 = junk_t[:, 1:2].broadcast_to([128, D])
        cw = sb("cwb", [128, 24], F32)
        gk = sb("gkb", [128, 8], F32)
        gb = sb("gbb", [128, 8], F32)
        esink = sb("esink", [128, 8], F32)
        flag = sb("flagb", [128, 1], F32)
        carry = sb("carry", [128, 8, 2], F32)
        xpre = g_post[0:4, :]
        xnpre = xn[0:4, :]
        preT = sb("preT", [128, 8, 4], BF16)

        psA = st.enter_context(nc.psum_tensor("psA", [128, 4096], F32))
        pT = psA[:, 3584:4096].bitcast(BF16).rearrange("p (a b) -> p a b", a=8)

        WA_in = arena[:, 0:32768].rearrange("p (kc n) -> p kc n", kc=8)
        WA_out = arena[:, 32768:40960].rearrange("p (kc n) -> p kc n", kc=8)
        WB_in = arena[:, 0:16384].rearrange("p (kc n) -> p kc n", kc=8)
        WB_out = arena[:, 16384:24576].rearrange("p (kc n) -> p kc n", kc=8)
        QT = arena[:, 24576:28672].rearrange("p (j n) -> p j n", j=8)
        szT = arena[:, 28672:32768].rearrange("p (j n) -> p j n", j=8)
        PT = [arena[:, 32768 + i * 4096:32768 + (i + 1) * 4096].rearrange("p (j n) -> p j n", j=8) for i in range(2)]
        yT = yT_u[:, :].rearrange("p (j n) -> p j n", j=8)
        biasT = yT
        c_sb0 = c_sb = interm[:, 0:514]
        vbuf0 = vbuf = interm[:, 514:1028]
        t10 = t1 = interm[:, 1028:1540]
        szb0 = szb = interm[:, 1540:2052]
        wkv_stage = otmp[:, :].rearrange("p (kc n) -> p kc n", kc=4)
        INTERM_KEYS = ["c_sb", "vbuf", "t1", "szb"]
        interm_h = sb("interm_h", [128, 4 * 132], F32)
        gT_all = interm[:, 0:1024].bitcast(BF16)
        gT = [gT_all[:, i * 1024:(i + 1) * 1024].rearrange("p (j n) -> p j n", j=8) for i in range(2)]
        rbuf = interm[:, 1024:1536].rearrange("p (a b) -> p a b", a=2)
        g1buf = interm[:, 1536:2048].rearrange("p (a b) -> p a b", a=2)

        P = Prog(nc)
        bank_ctr = [0]

        def bank():
            i = bank_ctr[0] % 7
            bank_ctr[0] += 1
            return i

        pair_ctr = [0]

        def pair():
            i = pair_ctr[0] % 3
            pair_ctr[0] += 1
            return i

        bankB_ctr = [0]
        BANKS_B = [0, 1, 2, 3, 6]

        def bankB():
            i = BANKS_B[bankB_ctr[0] % len(BANKS_B)]
            bankB_ctr[0] += 1
            return i

        def BK(i, n=512, off=0):
            return psA[:, i * 512 + off:i * 512 + off + n]

        def dma(eng, out, in_, sem, reads=(), writes=(), after=()):
            return P.op(eng, lambda e, o=out, i=in_: e.dma_start(out=o, in_=i), reads=reads, writes=writes, dma_sem=sem, after=after)

        P.op("dve", lambda e: e.memset(ss[:, :], 0.0), writes=["ss"])
        KT_ALL = lambda s_: ["kt%d_%d%d" % (s_, a_, b_) for a_ in range(2) for b_ in range(2)]
        P.op("dve", lambda e: e.memset(ktp[:, :, :, :], 0.0), writes=[k_ for s_ in range(5) for k_ in KT_ALL(s_)])
        P.op("dve", lambda e: e.memset(carry[:, :, :], 0.0), writes=["carry%d" % j for j in range(8)])
        P.op("dve", lambda e: e.memset(vp[:, :, :, :, :], 0.0), writes=["vp%d" % s_ for s_ in range(5)])

        def load_xgroup(g, after=()):
            b0 = 1 + 4 * g
            dma("sp", h[:, b0:b0 + 4, :], xmain_d[b0 * 128:(b0 + 4) * 128, :].rearrange("(b p) n -> p b n", p=128),
                "l_xg%d" % g, writes=["h%d" % (b0 + i) for i in range(4)], after=after)

        dma("sp", xpre, xpre_d, "l_xpre", writes=["g_post"])
        xg0_ops = []
        for i_ in range(4):
            xg0_ops.append(dma("sp", h[:, 1 + i_, :], xmain_d[(1 + i_) * 128:(2 + i_) * 128, :], "l_xg0_%d" % i_,
                               writes=["h%d" % (1 + i_)]))
        dma("sp", cw[:, :], cw_d, "l_cw", writes=["cw"])
        dma("sp", gk[:, :], gk_d, "l_gk", writes=["gk"])
        dma("sp", gb[:, :], gb_d, "l_gb", writes=["gb"])
        w_kv_v = w_kv_d.rearrange("(kc p) n -> p kc n", p=128)
        dma("sp", wkv_stage, w_kv_v[:, 0:4, :], "l_wkv0", writes=["otmp"])
        dma("sp", h[:, 0, :], xmain_d[0:128, :], "l_xh", writes=["h0"])
        dma("sp", flag[:, :], flag_d, "l_flag", writes=["flag"])
        dma("sp", esink[:, :], sinks_d, "l_sinks", writes=["esink"])

        dma("pool", ident[:, :], ident_d, "l_ident", writes=["ident"])
        dma("pool", g_pre[:, :], g_apre_d, "l_gpre", writes=["g_pre"])
        a_in_v = a_w_in_d.rearrange("(kc p) (part n) -> p kc part n", p=128, part=4)
        WA_in_v = arena[:, 0:32768].rearrange("p (kc part n) -> p kc part n", kc=8, part=4)
        wa_ops = {}
        for j in range(8):
            for part in (1, 2, 3, 0):
                wa_ops[j] = dma("pool", WA_in_v[:, :, part, j * 128:(j + 1) * 128], a_in_v[:, :, part, j * 128:(j + 1) * 128],
                                "l_wain%d" % j, writes=["WA_in%d_%d" % (j, part)], after=[xg0_ops[-1]] if j == 2 else ())
        waout_op = dma("pool", WA_out, a_w_out_d.rearrange("(kc p) n -> p kc n", p=128), "l_waout", writes=["WA_out"])
        dma("pool", onesp[:, :, :], onesp_d.rearrange("p (a b) -> p a b", a=2), "l_onesp", writes=["onesp"])

        col_ctr = [0]

        def newcols(n):
            c = col_ctr[0]
            col_ctr[0] += n
            assert col_ctr[0] <= NCOL
            return c

        def rstd_cols(c0, n, in_keys):
            P.op("act", lambda e: e.activation(out=rs[:, c0:c0 + n], in_=ss[:, c0:c0 + n], func=AF.Ln, scale=1.0 / D, bias=EPS),
                 reads=in_keys, writes=["rs%d" % c for c in range(c0, c0 + n)])
            P.op("act", lambda e: e.activation(out=rs[:, c0:c0 + n], in_=rs[:, c0:c0 + n], func=AF.Exp, scale=-0.5),
                 reads=["rs%d" % c for c in range(c0, c0 + n)], writes=["rs%d" % c for c in range(c0, c0 + n)])

        phase = ["A"]

        def norm_stats(blks):
            c0 = newcols(len(blks))
            for i, blk in enumerate(blks):
                if phase[0] == "B":
                    P.op("dve", lambda e, blk=blk, i=i: e.scalar_tensor_tensor(
                        out=xn2, in0=h[:, blk, :], scalar=1.0, in1=h[:, blk, :], op0=ALU.mult, op1=ALU.mult,
                        accum_out=ss[:, c0 + i:c0 + i + 1]), reads=["h%d" % blk, "ss"], writes=["junk", "ss%d" % (c0 + i)])
                    continue
                P.op("act", lambda e, blk=blk, i=i: e.activation(out=junk, in_=h[:, blk, :], func=AF.Square,
                                                                 accum_out=ss[:, c0 + i:c0 + i + 1]),
                     reads=["h%d" % blk, "ss"], writes=["junk", "ss%d" % (c0 + i)])
            rstd_cols(c0, len(blks), ["ss%d" % (c0 + i) for i in range(len(blks))])
            return c0

        xn_par = [0, 0]

        def norm_apply_dve(blk, col):
            p = xn_par[0] % 2
            xn_par[0] += 1
            P.op("dve", lambda e: e.scalar_tensor_tensor(
                out=xn_t[p][:, :], in0=h[:, blk, :], scalar=rs[:, col:col + 1], in1=g_pre[:, :],
                op0=ALU.mult, op1=ALU.mult), reads=["h%d" % blk, "rs%d" % col, "g_pre"], writes=["xn%d" % p])

        def norm_apply_pe(i):
            p = xn_par[1] % 2
            xn_par[1] += 1
            for kc in range(8):
                P.op("pe", lambda e, kc=kc: e.transpose(out=pT[:, kc, :], in_=xn_t[p][:, kc * 128:(kc + 1) * 128], identity=ident[:, :]),
                     reads=["xn%d" % p, "ident"], writes=["ps7"])
            eng = "dve" if phase[0] == "B" else "act"
            P.op(eng, lambda e: (e.tensor_copy if eng == "dve" else e.copy)(out=xT[:, :, 2 + i * 128:2 + (i + 1) * 128], in_=pT[:, :, :]),
                 reads=["ps7"], writes=["xT"])

        def norm_apply(blk, col, i):
            norm_apply_dve(blk, col)
            norm_apply_pe(i)

        def post_norm_residual(pi, blk):
            c = newcols(1)
            src = psA[:, pi * 1024:(pi + 1) * 1024]
            pk = ["ps%d" % (2 * pi), "ps%d" % (2 * pi + 1)]
            P.op("act", lambda e: e.activation(out=junk, in_=src, func=AF.Square, accum_out=ss[:, c:c + 1]),
                 reads=pk + ["ss"], writes=["junk", "ss%d" % c])
            rstd_cols(c, 1, ["ss%d" % c])
            P.op("dve", lambda e: e.scalar_tensor_tensor(out=otmp[:, :], in0=src, scalar=rs[:, c:c + 1], in1=g_post[:, :],
                                                         op0=ALU.mult, op1=ALU.mult),
                 reads=pk + ["rs%d" % c, "g_post"], writes=["otmp"])
            add_eng = "dve" if (phase[0] == "B" and blk == NB1 - 1) else "pool"
            P.op(add_eng, lambda e: e.tensor_tensor(out=h[:, blk, :], in0=h[:, blk, :], in1=otmp[:, :], op=ALU.add),
                 reads=["otmp", "h%d" % blk], writes=["h%d" % blk])

        groups = [[0]] + [[1 + 4 * g + i for i in range(4)] for g in range(NGRP)]

        def pre_stats():
            c = newcols(1)
            P.op("act", lambda e: e.activation(out=junk_t[0:4, 0:1].broadcast_to([4, D]), in_=xpre, func=AF.Square, accum_out=ss[0:4, c:c + 1]),
                 reads=["g_post", "ss"], writes=["junk", "ss%d" % c])
            rstd_cols(c, 1, ["ss%d" % c])
            return c

        def pre_apply(c):
            P.op("dve", lambda e: e.scalar_tensor_tensor(out=xnpre, in0=xpre, scalar=rs[0:4, c:c + 1],
                                                         in1=g_pre[0:4, :], op0=ALU.mult, op1=ALU.mult),
                 reads=["g_post", "rs%d" % c, "g_pre"], writes=["xn0"])
            for kc in range(8):
                P.op("pe", lambda e, kc=kc: e.transpose(out=pT[:, kc, 0:4], in_=xnpre[:, kc * 128:(kc + 1) * 128], identity=ident[0:4, 0:4]),
                     reads=["xn0", "ident"], writes=["ps7"])
            P.op("act", lambda e: e.copy(out=preT[:, :, :], in_=pT[:, :, 0:4]), reads=["ps7"], writes=["preT"])

        def set_left_cols(which):
            P.op("dve", lambda e: e.tensor_copy(out=xT[:, :, 0:2], in_=preT[:, :, 2 * which:2 * which + 2]),
                 reads=["preT"], writes=["xT"])

        def kv_fold_prep():
            P.op("dve", lambda e: e.reciprocal(out=gb[:, :], in_=gb[:, :]), reads=["gb"], writes=["gb"])
            P.op("dve", lambda e: e.tensor_tensor(out=gk[:, :], in0=gk[:, :], in1=gb[:, :], op=ALU.mult), reads=["gk", "gb"], writes=["gk"])

        def kv_fold(kc):
            k4 = kc % 4
            if kc == 4:
                dma("sp", wkv_stage, w_kv_v[:, 4:8, :], "l_wkv1", writes=["otmp"])
            P.op("act", lambda e: e.activation(out=wkp[:, kc, :], in_=wkv_stage[:, k4, 0:128], func=AF.Copy,
                                               scale=gk[:, kc:kc + 1]), reads=["otmp", "gk"], writes=["wkp"])
            P.op("act", lambda e: e.activation(out=wv[:, kc, :], in_=wkv_stage[:, k4, 128:256], func=AF.Copy,
                                               scale=gk[:, kc:kc + 1]), reads=["otmp", "gk"], writes=["wv"])
            if kc == 7:
                dma("sp", g_post[:, :], g_apost_d, "l_gpost", writes=["g_post"])
                for g_ in (1, 2, 3):
                    load_xgroup(g_, after=[waout_op])

        def a_inproj_chunks(blks, mode, hook=None):
            halo = mode == "halo"
            NT = 128 * len(blks)
            lo = 0 if halo else 2
            ncu = NT + 2 - lo

            def chunk(j):
                if halo and j % 2 == 1:
                    c_sb, vbuf, t1, szb = (interm_h[:, 132 * k_:132 * (k_ + 1)] for k_ in range(4))
                    KS = ["c_sbh", "vbufh", "t1h", "szbh"]
                else:
                    c_sb, vbuf, t1, szb = c_sb0, vbuf0, t10, szb0
                    KS = ["c_sb", "vbuf", "t1", "szb"]
                _inproj_chunk(j, blks, mode, halo, NT, lo, ncu, c_sb, vbuf, t1, szb, KS, hook)

            return chunk

        def a_inproj(blks, mode, hook=None):
            bank_ctr[0] = 0
            ch = a_inproj_chunks(blks, mode, hook)
            for j in range(8):
                ch(j)

        fixed_banks = [None]

        def _inproj_chunk(j, blks, mode, halo, NT, lo, ncu, c_sb, vbuf, t1, szb, KS, hook):
            if True:
                bks = {}
                for pidx_, (part, nm) in enumerate(((1, "c"), (2, "u"), (3, "z"), (0, "b"))):
                    bi = fixed_banks[0][pidx_] if fixed_banks[0] is not None else bank()
                    bks[nm] = bi
                    cu = nm in ("c", "u")
                    a = lo if cu else 2
                    n = ncu if cu else NT
                    for kc in range(8):
                        P.op("pe", lambda e, bi=bi, kc=kc, part=part, a=a, n=n, j=j: e.matmul(
                            BK(bi, n), lhsT=WA_in[:, kc, part * 1024 + j * 128:part * 1024 + (j + 1) * 128],
                            rhs=xT[:, kc, a:a + n], start=(kc == 0), stop=(kc == 7)),
                            reads=["xT"] + ["WA_in%d_%d" % (j, p_) for p_ in range(4)], writes=["ps%d" % bi])
                bc, bu, bz, bb = bks["c"], bks["u"], bks["z"], bks["b"]
                if mode == "first":
                    bm = bank()
                    for pi_, part in enumerate((1, 2)):
                        for kc in range(8):
                            P.op("pe", lambda e, bm=bm, kc=kc, part=part, pi_=pi_, j=j: e.matmul(
                                BK(bm, 2, 2 * pi_), lhsT=WA_in[:, kc, part * 1024 + j * 128:part * 1024 + (j + 1) * 128],
                                rhs=xT[:, kc, 0:2], start=(kc == 0), stop=(kc == 7)),
                                reads=["xT"] + ["WA_in%d_%d" % (j, p_) for p_ in range(4)], writes=["ps%d" % bm])
                P.op("act", lambda e, bc=bc: e.copy(out=c_sb[:, 0:ncu], in_=BK(bc, ncu)), reads=["ps%d" % bc], writes=[KS[0]])
                if mode == "first":
                    P.op("act", lambda e, bm=bm: e.copy(out=c_sb[:, 512:514], in_=BK(bm, 2, 0)), reads=["ps%d" % bm], writes=[KS[0]])
                    P.op("dve", lambda e, bm=bm: e.tensor_tensor(out=vbuf[:, 0:2], in0=c_sb[:, 512:514], in1=BK(bm, 2, 2), op=ALU.mult),
                         reads=[KS[0], "ps%d" % bm], writes=[KS[1]])
                elif mode == "normal":
                    P.op("dve", lambda e, j=j: e.tensor_copy(out=vbuf[:, 0:2], in_=carry[:, j, :]),
                         reads=["carry%d" % j], writes=[KS[1]])
                P.op("dve", lambda e, bu=bu: e.tensor_tensor(out=vbuf[:, lo:lo + ncu], in0=c_sb[:, 0:ncu], in1=BK(bu, ncu), op=ALU.mult),
                     reads=[KS[0], "ps%d" % bu], writes=[KS[1]])
                if not halo:
                    P.op("dve", lambda e, j=j: e.tensor_copy(out=carry[:, j, :], in_=vbuf[:, NT:NT + 2]),
                         reads=[KS[1]], writes=["carry%d" % j])
                P.op("act", lambda e, j=j: e.activation(out=t1[:, 0:NT], in_=vbuf[:, 2:2 + NT], func=AF.Copy,
                                                        scale=cw[:, 3 * j + 2:3 * j + 3]), reads=[KS[1], "cw"], writes=[KS[2]])
                P.op("dve", lambda e, j=j: e.scalar_tensor_tensor(out=t1[:, 0:NT], in0=vbuf[:, 1:1 + NT], scalar=cw[:, 3 * j + 1:3 * j + 2],
                                                                  in1=t1[:, 0:NT], op0=ALU.mult, op1=ALU.add),
                     reads=[KS[1], "cw", KS[2]], writes=[KS[2]])
                P.op("dve", lambda e, j=j: e.scalar_tensor_tensor(out=t1[:, 0:NT], in0=vbuf[:, 0:NT], scalar=cw[:, 3 * j:3 * j + 1],
                                                                  in1=t1[:, 0:NT], op0=ALU.mult, op1=ALU.add),
                     reads=[KS[1], "cw", KS[2]], writes=[KS[2]])
                P.op("act", lambda e, bz=bz: e.activation(out=szb[:, 0:NT], in_=BK(bz, NT), func=AF.Silu),
                     reads=["ps%d" % bz], writes=[KS[3]])
                P.op("dve", lambda e: e.tensor_tensor(out=szb[:, 0:NT], in0=szb[:, 0:NT], in1=t1[:, 0:NT], op=ALU.mult),
                     reads=[KS[3], KS[2]], writes=[KS[3]])
                P.op("dve", lambda e, bb=bb, j=j: e.tensor_tensor(out=yT[:, j, 0:NT], in0=szb[:, 0:NT], in1=BK(bb, NT), op=ALU.mult),
                     reads=[KS[3], "ps%d" % bb], writes=["yT%d_%d" % (j, b_) for b_ in range(len(blks))])
                if hook is not None:
                    hook(j)

        def a_outproj(blk, i, pi):
            for kc in range(8):
                for half in range(2):
                    bi = 2 * pi + half
                    P.op("pe", lambda e, bi=bi, kc=kc, half=half: e.matmul(
                        BK(bi), lhsT=yT[:, kc, i * 128:(i + 1) * 128], rhs=WA_out[:, kc, half * 512:(half + 1) * 512],
                        start=(kc == 0), stop=(kc == 7)), reads=["yT%d_%d" % (kc, i), "WA_out"], writes=["ps%d" % bi])
            post_norm_residual(pi, blk)

        b_in_v = b_w_in_d.rearrange("(kc p) n -> p kc n", p=128)

        def emit_b_weights(after_op):
            for qi in range(4):
                dma("pool", WB_in[:, :, qi * 512:(qi + 1) * 512], b_in_v[:, :, qi * 512:(qi + 1) * 512], "l_wbin%d" % qi,
                    writes=["WB_in%d" % qi], after=[after_op])
            dma("pool", WB_out, b_w_out_d.rearrange("(kc p) n -> p kc n", p=128), "l_wbout", writes=["WB_out"], after=[after_op])
            P.op("act", lambda e: e.activation(out=esink[:, :], in_=esink[:, :], func=AF.Exp), reads=["esink"], writes=["esink"])

        def emit_b_tables():
            dma("pool", biasT.rearrange("p j n -> p (j n)"), bias_d, "l_bias", writes=["yT%d_%d" % (j, b_) for j in range(8) for b_ in range(4)] + ["biasT"])
            dma("sp", g_post[:, :], g_bpost_d, "l_gbpost", writes=["g_post"])

        out_ops = []
        colsB = {}

        def b_kv(blks, halo):
            NT = 128 * len(blks)
            s0 = 0 if halo else 1
            bi = bank()
            for kc in range(8):
                P.op("pe", lambda e, bi=bi, kc=kc: e.matmul(
                    BK(bi, NT), lhsT=wkp[:, kc, :], rhs=xT[:, kc, 2:2 + NT], start=(kc == 0), stop=(kc == 7)),
                    reads=["xT", "wkp"], writes=["ps%d" % bi])
            P.op("dve", lambda e, bi=bi: e.tensor_copy(out=kt_nat[:, 0:NT], in_=BK(bi, NT)), reads=["ps%d" % bi], writes=["kt_nat"])
            for kvh in range(2):
                for hh in range(2):
                    slot_keys = ["kt%d_%d%d" % (s0 + i, kvh, hh) for i in range(len(blks))]
                    dst = ktp[hh * 64:(hh + 1) * 64, kvh, hh, s0 * 128:s0 * 128 + NT]
                    src = kt_nat[kvh * 64:(kvh + 1) * 64, 0:NT]
                    if kvh == hh:
                        P.op("act", lambda e, dst=dst, src=src: e.copy(out=dst, in_=src), reads=["kt_nat"], writes=slot_keys)
                    else:
                        dma("sp", dst, src, "l_kt%d" % kvh, reads=["kt_nat"], writes=slot_keys)
            for i in range(len(blks)):
                bi = bank()
                for kc in range(8):
                    P.op("pe", lambda e, bi=bi, kc=kc, i=i: e.matmul(
                        BK(bi, 128), lhsT=xT[:, kc, 2 + i * 128:2 + (i + 1) * 128], rhs=wv[:, kc, :], start=(kc == 0), stop=(kc == 7)),
                        reads=["xT", "wv"], writes=["ps%d" % bi])
                for hh in range(2):
                    P.op("act", lambda e, bi=bi, i=i, hh=hh: e.copy(
                        out=vp[:, s0 + i, :, hh, hh * 64:(hh + 1) * 64], in_=BK(bi, 128).rearrange("p (a b) -> p a b", a=2)),
                        reads=["ps%d" % bi], writes=["vp%d" % (s0 + i)])

        def b_qz():
            for j in range(8):
                bi = bank()
                for kc in range(8):
                    P.op("pe", lambda e, bi=bi, kc=kc, j=j: e.matmul(
                        BK(bi), lhsT=WB_in[:, kc, j * 128:(j + 1) * 128], rhs=xT[:, kc, 2:514], start=(kc == 0), stop=(kc == 7)),
                        reads=["xT", "WB_in%d" % (j // 4)], writes=["ps%d" % bi])
                P.op("dve", lambda e, bi=bi, j=j: e.tensor_scalar(out=QT[:, j, :], in0=BK(bi), scalar1=0.125, scalar2=None, op0=ALU.mult),
                     reads=["ps%d" % bi], writes=["QT%d" % j])
            for j in range(8):
                bi = bank()
                for kc in range(8):
                    P.op("pe", lambda e, bi=bi, kc=kc, j=j: e.matmul(
                        BK(bi), lhsT=WB_in[:, kc, 1024 + j * 128:1024 + (j + 1) * 128], rhs=xT[:, kc, 2:514], start=(kc == 0), stop=(kc == 7)),
                        reads=["xT", "WB_in%d" % (2 + j // 4)], writes=["ps%d" % bi])
                P.op("act", lambda e, bi=bi, j=j: e.activation(out=szT[:, j, :], in_=BK(bi), func=AF.Silu),
                     reads=["ps%d" % bi], writes=["szT%d" % j])

        def b_scores(j, i, par, first_global):
            kvh = j // 4
            bi = bankB()
            P.op("pe", lambda e: e.matmul(BK(bi), lhsT=ident[:, :], rhs=biasT[:, j, :], start=True, stop=False),
                 reads=["ident", "biasT"], writes=["ps%d" % bi])
            for hh in range(2):
                for kb in range(2):
                    slot = i + kb
                    P.op("pe", lambda e, hh=hh, kb=kb, slot=slot: e.matmul(
                        BK(bi, 128, (hh * 2 + kb) * 128), lhsT=ktp[:, kvh, hh, slot * 128:(slot + 1) * 128],
                        rhs=QT[:, j, i * 128:(i + 1) * 128], start=False, stop=(hh == 1 and kb == 1)),
                        reads=["kt%d_%d%d" % (slot, kvh, hh), "QT%d" % j], writes=["ps%d" % bi])
            P.op("act", lambda e: e.activation(out=PT[par][:, j, :], in_=BK(bi), func=AF.Exp),
                 reads=["ps%d" % bi], writes=["PT%d_%d" % (par, j)])
            if first_global:
                view = PT[par][:, j, :].rearrange("p (hh kb q) -> p hh kb q", hh=2, kb=2)[:, :, 0, :]
                P.op("dve", lambda e: e.tensor_scalar(out=view, in0=view, scalar1=flag[:, 0:1], scalar2=None, op0=ALU.mult),
                     reads=["flag", "PT%d_%d" % (par, j)], writes=["PT%d_%d" % (par, j)])

        od_state = {}

        def b_pv(j, i, par):
            kvh = j // 4
            q = j % 2
            if q == 0:
                od_state["bank"] = bankB()
            bi = od_state["bank"]
            rp = (j // 2) % 2
            pv = PT[par][:, j, :].rearrange("p (hh kb q) -> p hh kb q", hh=2, kb=2)
            for which in range(2):
                n = 0
                for hh in range(2):
                    for kb in range(2):
                        slot = i + kb
                        lhs = vp[:, slot, kvh, hh, :] if which == 0 else onesp[:, hh, :]
                        P.op("pe", lambda e, lhs=lhs, hh=hh, kb=kb, which=which, n=n: e.matmul(
                            BK(bi, 128, which * 256 + q * 128), lhsT=lhs, rhs=pv[:, hh, kb, :], start=(n == 0), stop=(n == 3)),
                            reads=["vp%d" % slot, "onesp", "PT%d_%d" % (par, j)], writes=["ps%d" % bi])
                        n += 1
            if q == 0:
                return
            P.op("dve", lambda e: e.tensor_tensor(
                out=g1buf[:, rp, :].rearrange("p (a b) -> p a b", a=2), in0=BK(bi, 256, 0).rearrange("p (a b) -> p a b", a=2),
                in1=szT[:, j - 1:j + 1, i * 128:(i + 1) * 128], op=ALU.mult),
                reads=["ps%d" % bi, "szT%d" % (j - 1), "szT%d" % j], writes=["g1buf%d" % rp, "szb"])
            for qq in range(2):
                jj = j - 1 + qq
                P.op("dve", lambda e, qq=qq, jj=jj: e.tensor_scalar(
                    out=rbuf[:, rp, qq * 128:(qq + 1) * 128], in0=BK(bi, 128, 256 + qq * 128), scalar1=esink[:, jj:jj + 1],
                    scalar2=None, op0=ALU.add), reads=["ps%d" % bi, "esink"], writes=["rbuf%d" % rp, "t1"])
            P.op("act", lambda e: e.activation(out=rbuf[:, rp, :], in_=rbuf[:, rp, :], func=AF.Ln),
                 reads=["rbuf%d" % rp], writes=["rbuf%d" % rp])
            P.op("act", lambda e: e.activation(out=rbuf[:, rp, :], in_=rbuf[:, rp, :], func=AF.Exp, scale=-1.0),
                 reads=["rbuf%d" % rp], writes=["rbuf%d" % rp])
            P.op("pool", lambda e: e.tensor_tensor(
                out=gT[par][:, j - 1:j + 1, :], in0=g1buf[:, rp, :].rearrange("p (a b) -> p a b", a=2),
                in1=rbuf[:, rp, :].rearrange("p (a b) -> p a b", a=2), op=ALU.mult),
                reads=["g1buf%d" % rp, "rbuf%d" % rp],
                writes=["gT%d_%d" % (par, j - 1), "gT%d_%d" % (par, j), "c_sb", "vbuf"])

        def b_outproj(blk, par):
            pi = 2
            for kc in range(8):
                for half in range(2):
                    bi = 2 * pi + half
                    P.op("pe", lambda e, bi=bi, kc=kc, half=half: e.matmul(
                        BK(bi), lhsT=gT[par][:, kc, :], rhs=WB_out[:, kc, half * 512:(half + 1) * 512],
                        start=(kc == 0), stop=(kc == 7)), reads=["gT%d_%d" % (par, kc), "WB_out"], writes=["ps%d" % bi])
            post_norm_residual(pi, blk)
            ob = blk - 1
            out_ops.append(dma("sp", out_d[ob * 128:(ob + 1) * 128, :], h[:, blk, :], "s_out%d" % (ob // 4),
                               reads=["h%d" % blk]))

        orderA = [(groups[1], "first"), (groups[0], "halo")] + [(g_, "normal") for g_ in groups[2:]]
        cpre = pre_stats()
        pre_apply(cpre)
        set_left_cols(1)
        cols = {}
        cs_ = [norm_stats([blk]) for blk in orderA[0][0]]
        cols[0] = cs_[0]
        for i, blk in enumerate(orderA[0][0]):
            norm_apply(blk, cs_[i], i)
        kv_fold_prep()
        PAIR_SEQ = [2, 0, 1, 2]
        def make_hook(gi):
            nxt = orderA[gi + 1][0] if gi + 1 < len(orderA) else None
            nT = len(nxt) if nxt is not None else 0

            def hook(j):
                if nxt is None:
                    firstB = [groups[0][0]] + list(groups[1])
                    if j == 0:
                        dma("pool", g_pre[:, :], g_bpre_d, "l_gbpre", writes=["g_pre"])
                        colsB["c0"] = newcols(5)
                    if 1 <= j <= 5:
                        blk_ = firstB[j - 1]
                        c_ = colsB["c0"] + j - 1
                        P.op("dve", lambda e: e.scalar_tensor_tensor(
                            out=xn2, in0=h[:, blk_, :], scalar=1.0, in1=h[:, blk_, :], op0=ALU.mult, op1=ALU.mult,
                            accum_out=ss[:, c_:c_ + 1]), reads=["h%d" % blk_, "ss"], writes=["junk", "ss%d" % c_])
                    if j == 5:
                        rstd_cols(colsB["c0"], 5, ["ss%d" % (colsB["c0"] + k_) for k_ in range(5)])
                        colsB[0] = colsB["c0"]
                        colsB[1] = colsB["c0"] + 1
                    if j == 6:
                        norm_apply_dve(0, colsB[0])
                    if j == 7:
                        norm_apply_dve(groups[1][0], colsB[1])
                    return
                if gi == 0:
                    kv_fold(j)
                if j == 2:
                    cols[gi + 1] = norm_stats(nxt)
                j0_ = 3 if gi == 1 else 5
                if j == j0_:
                    norm_apply_dve(nxt[0], cols[gi + 1])
                if j == j0_ + 1 and nT > 1:
                    norm_apply_dve(nxt[1], cols[gi + 1] + 1)

            return hook

        for gi, (blks, mode) in enumerate(orderA):
            nxt = orderA[gi + 1][0] if gi + 1 < len(orderA) else None
            nxt_mode = orderA[gi + 1][1] if gi + 1 < len(orderA) else None
            nT = len(nxt) if nxt is not None else 0
            if gi != 1:
                a_inproj(blks, mode, make_hook(gi))
            if nxt_mode == "halo":
                set_left_cols(0)
            t_done = [0]

            def do_T(nxt=nxt, gi=gi, nT=nT, t_done=t_done):
                k = t_done[0]
                norm_apply_pe(k)
                if k + 2 < nT:
                    norm_apply_dve(nxt[k + 2], cols[gi + 1] + k + 2)
                t_done[0] += 1

            if gi == 0:
                do_T()
                hch = a_inproj_chunks(orderA[1][0], "halo", make_hook(1))
                plan = [("op", 0), ("h", 0), ("op", 1), ("h", 1), ("op", 2), ("h", 2), ("op", 3), ("h", 3),
                        ("h", 4), ("h", 5), ("h", 6), ("h", 7)]
                for kind, k in plan:
                    if kind == "op":
                        a_outproj(blks[k], k, 2 + k % 2)
                    else:
                        fixed_banks[0] = (0, 1, 2, 3) if (k < 4 or k % 2 == 1) else (4, 5, 6, 7)
                        hch(k)
                        fixed_banks[0] = None
                continue
            if nxt is None:
                last_inproj = [o for o in P.ops if o.eng == "pe"][-1]
                emit_b_weights(last_inproj)
                phase[0] = "B"
                g1b = groups[1]
                norm_apply_pe(0)
                norm_apply_dve(g1b[1], colsB[1] + 1)
                a_outproj(blks[0], 0, PAIR_SEQ[0])
                bank_ctr[0] = 2
                b_kv(groups[0], True)
                norm_apply_pe(0)
                norm_apply_dve(g1b[2], colsB[1] + 2)
                norm_apply_pe(1)
                norm_apply_dve(g1b[3], colsB[1] + 3)
                a_outproj(blks[1], 1, PAIR_SEQ[1])
                norm_apply_pe(2)
                a_outproj(blks[2], 2, PAIR_SEQ[2])
                norm_apply_pe(3)
                a_outproj(blks[3], 3, PAIR_SEQ[3])
                emit_b_tables()
                continue
            for _ in range(min(2, nT)):
                do_T()
            for i in range(len(blks)):
                a_outproj(blks[i], i, PAIR_SEQ[i % 4] if len(blks) > 1 else 2)
                if t_done[0] < nT:
                    do_T()
            while t_done[0] < nT:
                do_T()

        if debug:
            dbg_ops = [dma("sp", dbg_d.rearrange("(b p) n -> p b n", p=128), h[:, :, :], "s_dbg",
                           reads=["h%d" % b for b in range(NB1)])]
        else:
            dbg_ops = []

        LAG = 2
        for gi in range(1, len(groups)):
            blks = groups[gi]
            nxt = groups[gi + 1] if gi + 1 < len(groups) else None
            bank_ctr[0] = 0
            b_kv(blks, False)
            b_qz()
            if nxt is not None:
                colsB[gi + 1] = norm_stats(nxt)
            items = [(i, j) for i in range(len(blks)) for j in range(8)]
            nI = len(items)
            OP_DELAY = 2
            for t in range(nI + LAG + OP_DELAY):
                if t < nI:
                    i, j = items[t]
                    b_scores(j, i, i % 2, blks[i] == 1)
                    if nxt is not None and j == 0:
                        norm_apply_dve(nxt[i], colsB[gi + 1] + i)
                    if nxt is not None and j == 6:
                        norm_apply_pe(i)
                if LAG <= t < nI + LAG:
                    i, j = items[t - LAG]
                    b_pv(j, i, i % 2)
                    if t - LAG == nI - 1:
                        P.op("dve", lambda e: e.tensor_copy(out=ktp[:, :, :, 0:128], in_=ktp[:, :, :, 512:640]),
                             reads=KT_ALL(4), writes=KT_ALL(0))
                        P.op("dve", lambda e: e.tensor_copy(out=vp[:, 0, :, :, :], in_=vp[:, 4, :, :, :]), reads=["vp4"], writes=["vp0"])
                if LAG + OP_DELAY <= t:
                    i, j = items[t - LAG - OP_DELAY]
                    if j == 7:
                        b_outproj(blks[i], i % 2)

        finals = [out_ops[4 * g + 3] for g in range(NGRP)] + dbg_ops
        for o in out_ops:
            o.signal = True
        P.emit(final_waits=finals)
    nc._prog = P
    return nc


def _t5_bucket(dist):
    n_buckets, max_distance = 32, 128
    max_exact = n_buckets // 2
    d = np.maximum(dist, 1).astype(np.float32)
    large = max_exact + (np.log(d / np.float32(max_exact)) / np.float32(math.log(max_distance / max_exact))
                         * np.float32(n_buckets - max_exact)).astype(np.int32)
    large = np.minimum(large, n_buckets - 1)
    return np.where(dist < max_exact, dist, large)


def _bias_table(rel_bias):
    s = np.arange(128)[:, None]
    q = np.arange(128)[None, :]
    tab = np.empty((128, 8, 2, 2, 128), np.float32)
    rb = np.asarray(rel_bias, np.float32)
    for kb in range(2):
        dist = q - s + (128 if kb == 0 else 0)
        valid = (dist >= 0) & (dist < 128)
        bucket = _t5_bucket(np.clip(dist, 0, None).astype(np.int32))
        g = rb[bucket]
        g = np.where(valid[:, :, None], g, np.float32(MASK_VAL))
        tab[:, :, :, kb, :] = g.transpose(0, 2, 1).reshape(128, 8, 2, 128)
    return np.ascontiguousarray(tab.reshape(128, 4096))


_NC_CACHE = {}


def _host_inputs(x, a_pre_norm, a_w_in, a_conv_w, a_w_out, a_post_norm, kv_norm, w_kv, rel_bias,
                 b_pre_norm, b_w_in, b_sinks, b_w_out, b_post_norm):
    f = lambda a: np.ascontiguousarray(np.asarray(a, np.float32))
    x = f(x)
    bc = lambda v: np.ascontiguousarray(np.broadcast_to(f(v).reshape(1, D), (128, D)))
    percol = lambda v: np.ascontiguousarray(f(v).reshape(8, 128).T)
    cwl = np.ascontiguousarray(f(a_conv_w).reshape(3, 8, 128).transpose(2, 1, 0).reshape(128, 24))
    sk = f(b_sinks).reshape(8, 2)
    sinks_l = np.ascontiguousarray(np.repeat(sk.T, 64, axis=0))
    onesp = np.zeros((128, 2, 128), np.float32)
    onesp[:, 0, 0:64] = 1.0
    onesp[:, 1, 64:128] = 1.0
    shared = {
        "ident": np.eye(128, dtype=np.float32),
        "onesp": onesp.reshape(128, 256),
        "a_w_in": f(a_w_in).reshape(D, 4 * D),
        "a_w_out": f(a_w_out).reshape(D, D),
        "b_w_in": f(b_w_in).reshape(D, 2 * D),
        "b_w_out": f(b_w_out).reshape(D, D),
        "w_kv": f(w_kv),
        "g_apre": bc(a_pre_norm), "g_apost": bc(a_post_norm), "g_bpre": bc(b_pre_norm), "g_bpost": bc(b_post_norm),
        "gk": percol(kv_norm), "gb": percol(b_pre_norm),
        "cw": cwl, "sinks": sinks_l, "biasT": _bias_table(rel_bias),
    }
    in_maps = []
    for c in range(NCORES):
        b, qd = divmod(c, 4)
        start = qd * TOK_PER_CORE
        xs = np.zeros((130 + TOK_PER_CORE, D), np.float32)
        lo = start - 130
        if lo >= 0:
            xs[:] = x[b, lo:start + TOK_PER_CORE]
        else:
            xs[130:] = x[b, 0:TOK_PER_CORE]
        m = dict(shared)
        m["xpre"] = np.ascontiguousarray(np.concatenate([xs[0:2], xs[128:130]], axis=0))
        m["xmain"] = np.ascontiguousarray(xs[2:])
        m["flag"] = np.full((128, 1), 0.0 if qd == 0 else 1.0, np.float32)
        in_maps.append(m)
    return in_maps


def kernel(**inputs):
    in_maps = _host_inputs(**inputs)
    if "nc" not in _NC_CACHE:
        _NC_CACHE["nc"] = build_nc()
    nc = _NC_CACHE["nc"]
    res = run_bass_kernel_spmd(nc, in_maps, core_ids=list(range(NCORES)))
    out = np.empty((2, 4 * TOK_PER_CORE, D), np.float32)
    for c in range(NCORES):
        b, qd = divmod(c, 4)
        out[b, qd * TOK_PER_CORE:(qd + 1) * TOK_PER_CORE] = res.results[c]["out"]
    return out
```

```python
import math
from contextlib import ExitStack

import numpy as np
import concourse.bass as bass
import concourse.mybir as mybir
from concourse.bass_utils import run_bass_kernel_spmd

F32 = mybir.dt.float32
BF16 = mybir.dt.bfloat16
AF = mybir.ActivationFunctionType
ALU = mybir.AluOpType

D = 1024
NCORES = 8
TOK_PER_CORE = 2048
NBLK = 16
NB1 = NBLK + 1
NGRP = 4
EPS = 1e-6
MASK_VAL = -80.0
NCOL = 72

COMPUTE = ("pe", "act", "dve", "pool")


class _Op:
    __slots__ = ("eng", "fn", "deps", "signal", "count", "sem", "is_dma", "idx", "rw")

    def __init__(self, eng, fn, is_dma, sem):
        self.eng = eng
        self.fn = fn
        self.deps = []
        self.signal = False
        self.count = None
        self.sem = sem
        self.is_dma = is_dma


class Prog:
    def __init__(self, nc, same_engine_sync=True):
        self.nc = nc
        self.ops = []
        self.res = {}
        self.same_engine_sync = same_engine_sync

    def _r(self, k):
        r = self.res.get(k)
        if r is None:
            r = self.res[k] = [None, []]
        return r

    def op(self, eng, fn, reads=(), writes=(), dma_sem=None, after=()):
        o = _Op(eng, fn, dma_sem is not None, dma_sem)
        o.idx = len(self.ops)
        o.rw = (tuple(reads), tuple(writes))
        deps = {x.idx: x for x in after}
        for k in reads:
            r = self._r(k)
            if r[0] is not None:
                deps[r[0].idx] = r[0]
        for k in writes:
            r = self._r(k)
            if r[0] is not None:
                deps[r[0].idx] = r[0]
            for x in r[1]:
                deps[x.idx] = x
        for k in reads:
            self._r(k)[1].append(o)
        for k in writes:
            r = self._r(k)
            r[0] = o
            r[1] = []
        o.deps = list(deps.values())
        self.ops.append(o)
        return o

    def emit(self, final_waits=()):
        nc = self.nc

        def needs_sync(x, y):
            if x.is_dma or y.is_dma:
                return True
            if x.eng != y.eng:
                return True
            if x.eng == "pe":
                return False
            return self.same_engine_sync

        for y in self.ops:
            y.deps = [x for x in y.deps if needs_sync(x, y)]
            for x in y.deps:
                x.signal = True
        for o in final_waits:
            o.signal = True
        eng_cnt = {e: 0 for e in COMPUTE}
        dma_cnt = {}
        for o in self.ops:
            if not o.signal:
                continue
            if o.is_dma:
                dma_cnt[o.sem] = dma_cnt.get(o.sem, 0) + 16
                o.count = dma_cnt[o.sem]
            else:
                eng_cnt[o.eng] += 1
                o.count = eng_cnt[o.eng]
        sem_names = [e for e in COMPUTE if eng_cnt[e] > 0] + sorted(dma_cnt.keys())
        with ExitStack() as st:
            sems = {n: st.enter_context(nc.semaphore("s_" + n)) for n in sem_names}
            block = st.enter_context(nc.Block())

            def semof(o):
                return sems[o.sem] if o.is_dma else sems[o.eng]

            def run(engname):
                def body(eng):
                    waited = {}
                    for o in self.ops:
                        if o.eng != engname:
                            continue
                        need = {}
                        for x in o.deps:
                            key = x.sem if x.is_dma else x.eng
                            if need.get(key, (0, None))[0] < x.count:
                                need[key] = (x.count, x)
                        for key, (cnt, x) in need.items():
                            if waited.get(key, 0) >= cnt:
                                continue
                            waited[key] = cnt
                            eng.wait_ge(semof(x), cnt)
                        ins = o.fn(eng)
                        if o.signal:
                            ins.then_inc(semof(o), 16 if o.is_dma else 1)
                    if engname == "sp":
                        for o in final_waits:
                            eng.wait_ge(semof(o), o.count)

                return body

            block.tensor(run("pe"))
            block.scalar(run("act"))
            block.vector(run("dve"))
            block.gpsimd(run("pool"))
            block.sync(run("sp"))


def build_nc(debug=False):
    nc = bass.Bass("TRN2", target_bir_lowering=False)

    def din(name, shape, dt=F32):
        return nc.dram_tensor(name, list(shape), dt, kind="ExternalInput").ap()

    xpre_d = din("xpre", [4, D])
    xmain_d = din("xmain", [NB1 * 128, D])
    flag_d = din("flag", [128, 1])
    ident_d = din("ident", [128, 128])
    onesp_d = din("onesp", [128, 256])
    a_w_in_d = din("a_w_in", [D, 4 * D])
    a_w_out_d = din("a_w_out", [D, D])
    b_w_in_d = din("b_w_in", [D, 2 * D])
    b_w_out_d = din("b_w_out", [D, D])
    w_kv_d = din("w_kv", [D, 256])
    g_apre_d = din("g_apre", [128, D])
    g_apost_d = din("g_apost", [128, D])
    g_bpre_d = din("g_bpre", [128, D])
    g_bpost_d = din("g_bpost", [128, D])
    gk_d = din("gk", [128, 8])
    gb_d = din("gb", [128, 8])
    cw_d = din("cw", [128, 24])
    sinks_d = din("sinks", [128, 8])
    bias_d = din("biasT", [128, 4096])
    out_d = nc.dram_tensor("out", [TOK_PER_CORE, D], F32, kind="ExternalOutput").ap()
    dbg_d = None
    if debug:
        dbg_d = nc.dram_tensor("dbg", [NB1 * 128, D], F32, kind="ExternalOutput").ap()

    with ExitStack() as st:
        def sb(name, shape, dt):
            return st.enter_context(nc.sbuf_tensor(name, list(shape), dt))

        h = sb("h", [128, NB1, D], F32)
        arena = sb("arena", [128, 40960], BF16)
        xT = sb("xT", [128, 8, 514], BF16)
        wkp = sb("wkp", [128, 8, 128], BF16)
        kt_nat = sb("kt_nat", [128, 512], BF16)
        wv = sb("wv", [128, 8, 128], BF16)
        g_post = sb("g_post", [128, D], F32)
        otmp = sb("otmp", [128, D], F32)
        xn_t = [sb("xn%d" % i_, [128, D], BF16) for i_ in range(2)]
        xn = xn_t[0]
        g_pre = sb("g_pre", [128, D], BF16)
        interm = sb("interm", [128, 2052], F32)
        yT_u = sb("yT_u", [128, 4096], BF16)
        ktp = sb("ktp", [128, 2, 2, 640], BF16)
        vp = sb("vp", [128, 5, 2, 2, 128], BF16)
        ident = sb("identb", [128, 128], BF16)
        onesp = sb("onespb", [128, 2, 128], BF16)
        ss = sb("ss", [128, NCOL], F32)
        rs = sb("rs", [128, NCOL], F32)
        junk_t = sb("junk", [128, 2], BF16)
        junk = junk_t[:, 0:1].broadcast_to([128, D])
        xn2 = junk_t[:, 1:2].broadcast_to([128, D])
        cw = sb("cwb", [128, 24], F32)
        gk = sb("gkb", [128, 8], F32)
        gb = sb("gbb", [128, 8], F32)
        esink = sb("esink", [128, 8], F32)
        flag = sb("flagb", [128, 1], F32)
        carry = sb("carry", [128, 8, 2], F32)
        xpre = g_post[0:4, :]
        xnpre = xn[0:4, :]
        preT = sb("preT", [128, 8, 4], BF16)

        psA = st.enter_context(nc.psum_tensor("psA", [128, 4096], F32))
        pT = psA[:, 3584:4096].bitcast(BF16).rearrange("p (a b) -> p a b", a=8)

        WA_in = arena[:, 0:32768].rearrange("p (kc n) -> p kc n", kc=8)
        WA_out = arena[:, 32768:40960].rearrange("p (kc n) -> p kc n", kc=8)
        WB_in = arena[:, 0:16384].rearrange("p (kc n) -> p kc n", kc=8)
        WB_out = arena[:, 16384:24576].rearrange("p (kc n) -> p kc n", kc=8)
        QT = arena[:, 24576:28672].rearrange("p (j n) -> p j n", j=8)
        szT = arena[:, 28672:32768].rearrange("p (j n) -> p j n", j=8)
        PT = [arena[:, 32768 + i * 4096:32768 + (i + 1) * 4096].rearrange("p (j n) -> p j n", j=8) for i in range(2)]
        yT = yT_u[:, :].rearrange("p (j n) -> p j n", j=8)
        biasT = yT
        c_sb0 = c_sb = interm[:, 0:514]
        vbuf0 = vbuf = interm[:, 514:1028]
        t10 = t1 = interm[:, 1028:1540]
        szb0 = szb = interm[:, 1540:2052]
        wkv_stage = otmp[:, :].rearrange("p (kc n) -> p kc n", kc=4)
        INTERM_KEYS = ["c_sb", "vbuf", "t1", "szb"]
        interm_h = sb("interm_h", [128, 4 * 132], F32)
        gT_all = interm[:, 0:1024].bitcast(BF16)
        gT = [gT_all[:, i * 1024:(i + 1) * 1024].rearrange("p (j n) -> p j n", j=8) for i in range(2)]
        rbuf = interm[:, 1024:1536].rearrange("p (a b) -> p a b", a=2)
        g1buf = interm[:, 1536:2048].rearrange("p (a b) -> p a b", a=2)

        P = Prog(nc)
        bank_ctr = [0]

        def bank():
            i = bank_ctr[0] % 7
            bank_ctr[0] += 1
            return i

        pair_ctr = [0]

        def pair():
            i = pair_ctr[0] % 3
            pair_ctr[0] += 1
            return i

        bankB_ctr = [0]
        BANKS_B = [0, 1, 2, 3, 6]

        def bankB():
            i = BANKS_B[bankB_ctr[0] % len(BANKS_B)]
            bankB_ctr[0] += 1
            return i

        def BK(i, n=512, off=0):
            return psA[:, i * 512 + off:i * 512 + off + n]

        def dma(eng, out, in_, sem, reads=(), writes=(), after=()):
            return P.op(eng, lambda e, o=out, i=in_: e.dma_start(out=o, in_=i), reads=reads, writes=writes, dma_sem=sem, after=after)

        P.op("dve", lambda e: e.memset(ss[:, :], 0.0), writes=["ss"])
        KT_ALL = lambda s_: ["kt%d_%d%d" % (s_, a_, b_) for a_ in range(2) for b_ in range(2)]
        P.op("dve", lambda e: e.memset(ktp[:, :, :, :], 0.0), writes=[k_ for s_ in range(5) for k_ in KT_ALL(s_)])
        P.op("dve", lambda e: e.memset(carry[:, :, :], 0.0), writes=["carry%d" % j for j in range(8)])
        P.op("dve", lambda e: e.memset(vp[:, :, :, :, :], 0.0), writes=["vp%d" % s_ for s_ in range(5)])

        def load_xgroup(g, after=()):
            b0 = 1 + 4 * g
            dma("sp", h[:, b0:b0 + 4, :], xmain_d[b0 * 128:(b0 + 4) * 128, :].rearrange("(b p) n -> p b n", p=128),
                "l_xg%d" % g, writes=["h%d" % (b0 + i) for i in range(4)], after=after)

        dma("sp", xpre, xpre_d, "l_xpre", writes=["g_post"])
        xg0_ops = []
        for i_ in range(4):
            xg0_ops.append(dma("sp", h[:, 1 + i_, :], xmain_d[(1 + i_) * 128:(2 + i_) * 128, :], "l_xg0_%d" % i_,
                               writes=["h%d" % (1 + i_)]))
        dma("sp", cw[:, :], cw_d, "l_cw", writes=["cw"])
        dma("sp", gk[:, :], gk_d, "l_gk", writes=["gk"])
        dma("sp", gb[:, :], gb_d, "l_gb", writes=["gb"])
        w_kv_v = w_kv_d.rearrange("(kc p) n -> p kc n", p=128)
        dma("sp", wkv_stage, w_kv_v[:, 0:4, :], "l_wkv0", writes=["otmp"])
        dma("sp", h[:, 0, :], xmain_d[0:128, :], "l_xh", writes=["h0"])
        dma("sp", flag[:, :], flag_d, "l_flag", writes=["flag"])
        dma("sp", esink[:, :], sinks_d, "l_sinks", writes=["esink"])

        dma("pool", ident[:, :], ident_d, "l_ident", writes=["ident"])
        dma("pool", g_pre[:, :], g_apre_d, "l_gpre", writes=["g_pre"])
        a_in_v = a_w_in_d.rearrange("(kc p) (part n) -> p kc part n", p=128, part=4)
        WA_in_v = arena[:, 0:32768].rearrange("p (kc part n) -> p kc part n", kc=8, part=4)
        wa_ops = {}
        for j in range(8):
            for part in (1, 2, 3, 0):
                wa_ops[j] = dma("pool", WA_in_v[:, :, part, j * 128:(j + 1) * 128], a_in_v[:, :, part, j * 128:(j + 1) * 128],
                                "l_wain%d" % j, writes=["WA_in%d_%d" % (j, part)], after=[xg0_ops[-1]] if j == 3 else ())
        waout_op = dma("pool", WA_out, a_w_out_d.rearrange("(kc p) n -> p kc n", p=128), "l_waout", writes=["WA_out"])
        dma("pool", onesp[:, :, :], onesp_d.rearrange("p (a b) -> p a b", a=2), "l_onesp", writes=["onesp"])

        col_ctr = [0]

        def newcols(n):
            c = col_ctr[0]
            col_ctr[0] += n
            assert col_ctr[0] <= NCOL
            return c

        def rstd_cols(c0, n, in_keys):
            P.op("act", lambda e: e.activation(out=rs[:, c0:c0 + n], in_=ss[:, c0:c0 + n], func=AF.Ln, scale=1.0 / D, bias=EPS),
                 reads=in_keys, writes=["rs%d" % c for c in range(c0, c0 + n)])
            P.op("act", lambda e: e.activation(out=rs[:, c0:c0 + n], in_=rs[:, c0:c0 + n], func=AF.Exp, scale=-0.5),
                 reads=["rs%d" % c for c in range(c0, c0 + n)], writes=["rs%d" % c for c in range(c0, c0 + n)])

        phase = ["A"]

        def norm_stats(blks):
            c0 = newcols(len(blks))
            for i, blk in enumerate(blks):
                if phase[0] == "B":
                    P.op("dve", lambda e, blk=blk, i=i: e.scalar_tensor_tensor(
                        out=xn2, in0=h[:, blk, :], scalar=1.0, in1=h[:, blk, :], op0=ALU.mult, op1=ALU.mult,
                        accum_out=ss[:, c0 + i:c0 + i + 1]), reads=["h%d" % blk, "ss"], writes=["junk", "ss%d" % (c0 + i)])
                    continue
                P.op("act", lambda e, blk=blk, i=i: e.activation(out=junk, in_=h[:, blk, :], func=AF.Square,
                                                                 accum_out=ss[:, c0 + i:c0 + i + 1]),
                     reads=["h%d" % blk, "ss"], writes=["junk", "ss%d" % (c0 + i)])
            rstd_cols(c0, len(blks), ["ss%d" % (c0 + i) for i in range(len(blks))])
            return c0

        xn_par = [0, 0]

        def norm_apply_dve(blk, col):
            p = xn_par[0] % 2
            xn_par[0] += 1
            P.op("dve", lambda e: e.scalar_tensor_tensor(
                out=xn_t[p][:, :], in0=h[:, blk, :], scalar=rs[:, col:col + 1], in1=g_pre[:, :],
                op0=ALU.mult, op1=ALU.mult), reads=["h%d" % blk, "rs%d" % col, "g_pre"], writes=["xn%d" % p])

        def norm_apply_pe(i):
            p = xn_par[1] % 2
            xn_par[1] += 1
            for kc in range(8):
                P.op("pe", lambda e, kc=kc: e.transpose(out=pT[:, kc, :], in_=xn_t[p][:, kc * 128:(kc + 1) * 128], identity=ident[:, :]),
                     reads=["xn%d" % p, "ident"], writes=["ps7"])
            eng = "dve" if phase[0] == "B" else "act"
            P.op(eng, lambda e: (e.tensor_copy if eng == "dve" else e.copy)(out=xT[:, :, 2 + i * 128:2 + (i + 1) * 128], in_=pT[:, :, :]),
                 reads=["ps7"], writes=["xT"])

        def norm_apply(blk, col, i):
            norm_apply_dve(blk, col)
            norm_apply_pe(i)

        def post_norm_residual(pi, blk):
            c = newcols(1)
            src = psA[:, pi * 1024:(pi + 1) * 1024]
            pk = ["ps%d" % (2 * pi), "ps%d" % (2 * pi + 1)]
            P.op("act", lambda e: e.activation(out=junk, in_=src, func=AF.Square, accum_out=ss[:, c:c + 1]),
                 reads=pk + ["ss"], writes=["junk", "ss%d" % c])
            rstd_cols(c, 1, ["ss%d" % c])
            P.op("dve", lambda e: e.scalar_tensor_tensor(out=otmp[:, :], in0=src, scalar=rs[:, c:c + 1], in1=g_post[:, :],
                                                         op0=ALU.mult, op1=ALU.mult),
                 reads=pk + ["rs%d" % c, "g_post"], writes=["otmp"])
            add_eng = "dve" if (phase[0] == "B" and blk == NB1 - 1) else "pool"
            P.op(add_eng, lambda e: e.tensor_tensor(out=h[:, blk, :], in0=h[:, blk, :], in1=otmp[:, :], op=ALU.add),
                 reads=["otmp", "h%d" % blk], writes=["h%d" % blk])

        groups = [[0]] + [[1 + 4 * g + i for i in range(4)] for g in range(NGRP)]

        def pre_stats():
            c = newcols(1)
            P.op("act", lambda e: e.activation(out=junk_t[0:4, 0:1].broadcast_to([4, D]), in_=xpre, func=AF.Square, accum_out=ss[0:4, c:c + 1]),
                 reads=["g_post", "ss"], writes=["junk", "ss%d" % c])
            rstd_cols(c, 1, ["ss%d" % c])
            return c

        def pre_apply(c):
            P.op("dve", lambda e: e.scalar_tensor_tensor(out=xnpre, in0=xpre, scalar=rs[0:4, c:c + 1],
                                                         in1=g_pre[0:4, :], op0=ALU.mult, op1=ALU.mult),
                 reads=["g_post", "rs%d" % c, "g_pre"], writes=["xn0"])
            for kc in range(8):
                P.op("pe", lambda e, kc=kc: e.transpose(out=pT[:, kc, 0:4], in_=xnpre[:, kc * 128:(kc + 1) * 128], identity=ident[0:4, 0:4]),
                     reads=["xn0", "ident"], writes=["ps7"])
            P.op("act", lambda e: e.copy(out=preT[:, :, :], in_=pT[:, :, 0:4]), reads=["ps7"], writes=["preT"])

        def set_left_cols(which):
            P.op("dve", lambda e: e.tensor_copy(out=xT[:, :, 0:2], in_=preT[:, :, 2 * which:2 * which + 2]),
                 reads=["preT"], writes=["xT"])

        def kv_fold_prep():
            P.op("dve", lambda e: e.reciprocal(out=gb[:, :], in_=gb[:, :]), reads=["gb"], writes=["gb"])
            P.op("dve", lambda e: e.tensor_tensor(out=gk[:, :], in0=gk[:, :], in1=gb[:, :], op=ALU.mult), reads=["gk", "gb"], writes=["gk"])

        def kv_fold(kc):
            k4 = kc % 4
            if kc == 4:
                dma("sp", wkv_stage, w_kv_v[:, 4:8, :], "l_wkv1", writes=["otmp"])
            P.op("act", lambda e: e.activation(out=wkp[:, kc, :], in_=wkv_stage[:, k4, 0:128], func=AF.Copy,
                                               scale=gk[:, kc:kc + 1]), reads=["otmp", "gk"], writes=["wkp"])
            P.op("act", lambda e: e.activation(out=wv[:, kc, :], in_=wkv_stage[:, k4, 128:256], func=AF.Copy,
                                               scale=gk[:, kc:kc + 1]), reads=["otmp", "gk"], writes=["wv"])
            if kc == 7:
                dma("sp", g_post[:, :], g_apost_d, "l_gpost", writes=["g_post"])
                for g_ in (1, 2, 3):
                    load_xgroup(g_, after=[waout_op])

        def a_inproj(blks, mode, hook=None):
            halo = mode == "halo"
            NT = 128 * len(blks)
            bank_ctr[0] = 0
            lo = 0 if halo else 2
            ncu = NT + 2 - lo
            for j in range(8):
                if halo and j % 2 == 1:
                    c_sb, vbuf, t1, szb = (interm_h[:, 132 * k_:132 * (k_ + 1)] for k_ in range(4))
                    KS = ["c_sbh", "vbufh", "t1h", "szbh"]
                else:
                    c_sb, vbuf, t1, szb = c_sb0, vbuf0, t10, szb0
                    KS = ["c_sb", "vbuf", "t1", "szb"]
                _inproj_chunk(j, blks, mode, halo, NT, lo, ncu, c_sb, vbuf, t1, szb, KS, hook)

        def _inproj_chunk(j, blks, mode, halo, NT, lo, ncu, c_sb, vbuf, t1, szb, KS, hook):
            if True:
                bks = {}
                for part, nm in ((1, "c"), (2, "u"), (3, "z"), (0, "b")):
                    bi = bank()
                    bks[nm] = bi
                    cu = nm in ("c", "u")
                    a = lo if cu else 2
                    n = ncu if cu else NT
                    for kc in range(8):
                        P.op("pe", lambda e, bi=bi, kc=kc, part=part, a=a, n=n, j=j: e.matmul(
                            BK(bi, n), lhsT=WA_in[:, kc, part * 1024 + j * 128:part * 1024 + (j + 1) * 128],
                            rhs=xT[:, kc, a:a + n], start=(kc == 0), stop=(kc == 7)),
                            reads=["xT"] + ["WA_in%d_%d" % (j, p_) for p_ in range(4)], writes=["ps%d" % bi])
                bc, bu, bz, bb = bks["c"], bks["u"], bks["z"], bks["b"]
                if mode == "first":
                    bm = bank()
                    for pi_, part in enumerate((1, 2)):
                        for kc in range(8):
                            P.op("pe", lambda e, bm=bm, kc=kc, part=part, pi_=pi_, j=j: e.matmul(
                                BK(bm, 2, 2 * pi_), lhsT=WA_in[:, kc, part * 1024 + j * 128:part * 1024 + (j + 1) * 128],
                                rhs=xT[:, kc, 0:2], start=(kc == 0), stop=(kc == 7)),
                                reads=["xT"] + ["WA_in%d_%d" % (j, p_) for p_ in range(4)], writes=["ps%d" % bm])
                P.op("act", lambda e, bc=bc: e.copy(out=c_sb[:, 0:ncu], in_=BK(bc, ncu)), reads=["ps%d" % bc], writes=[KS[0]])
                if mode == "first":
                    P.op("act", lambda e, bm=bm: e.copy(out=c_sb[:, 512:514], in_=BK(bm, 2, 0)), reads=["ps%d" % bm], writes=[KS[0]])
                    P.op("dve", lambda e, bm=bm: e.tensor_tensor(out=vbuf[:, 0:2], in0=c_sb[:, 512:514], in1=BK(bm, 2, 2), op=ALU.mult),
                         reads=[KS[0], "ps%d" % bm], writes=[KS[1]])
                elif mode == "normal":
                    P.op("dve", lambda e, j=j: e.tensor_copy(out=vbuf[:, 0:2], in_=carry[:, j, :]),
                         reads=["carry%d" % j], writes=[KS[1]])
                P.op("dve", lambda e, bu=bu: e.tensor_tensor(out=vbuf[:, lo:lo + ncu], in0=c_sb[:, 0:ncu], in1=BK(bu, ncu), op=ALU.mult),
                     reads=[KS[0], "ps%d" % bu], writes=[KS[1]])
                if not halo:
                    P.op("dve", lambda e, j=j: e.tensor_copy(out=carry[:, j, :], in_=vbuf[:, NT:NT + 2]),
                         reads=[KS[1]], writes=["carry%d" % j])
                P.op("act", lambda e, j=j: e.activation(out=t1[:, 0:NT], in_=vbuf[:, 2:2 + NT], func=AF.Copy,
                                                        scale=cw[:, 3 * j + 2:3 * j + 3]), reads=[KS[1], "cw"], writes=[KS[2]])
                P.op("dve", lambda e, j=j: e.scalar_tensor_tensor(out=t1[:, 0:NT], in0=vbuf[:, 1:1 + NT], scalar=cw[:, 3 * j + 1:3 * j + 2],
                                                                  in1=t1[:, 0:NT], op0=ALU.mult, op1=ALU.add),
                     reads=[KS[1], "cw", KS[2]], writes=[KS[2]])
                P.op("dve", lambda e, j=j: e.scalar_tensor_tensor(out=t1[:, 0:NT], in0=vbuf[:, 0:NT], scalar=cw[:, 3 * j:3 * j + 1],
                                                                  in1=t1[:, 0:NT], op0=ALU.mult, op1=ALU.add),
                     reads=[KS[1], "cw", KS[2]], writes=[KS[2]])
                P.op("act", lambda e, bz=bz: e.activation(out=szb[:, 0:NT], in_=BK(bz, NT), func=AF.Silu),
                     reads=["ps%d" % bz], writes=[KS[3]])
                P.op("dve", lambda e: e.tensor_tensor(out=szb[:, 0:NT], in0=szb[:, 0:NT], in1=t1[:, 0:NT], op=ALU.mult),
                     reads=[KS[3], KS[2]], writes=[KS[3]])
                P.op("dve", lambda e, bb=bb, j=j: e.tensor_tensor(out=yT[:, j, 0:NT], in0=szb[:, 0:NT], in1=BK(bb, NT), op=ALU.mult),
                     reads=[KS[3], "ps%d" % bb], writes=["yT%d" % j])
                if hook is not None:
                    hook(j)

        def a_outproj(blk, i, pi):
            for kc in range(8):
                for half in range(2):
                    bi = 2 * pi + half
                    P.op("pe", lambda e, bi=bi, kc=kc, half=half: e.matmul(
                        BK(bi), lhsT=yT[:, kc, i * 128:(i + 1) * 128], rhs=WA_out[:, kc, half * 512:(half + 1) * 512],
                        start=(kc == 0), stop=(kc == 7)), reads=["yT%d" % kc, "WA_out"], writes=["ps%d" % bi])
            post_norm_residual(pi, blk)

        b_in_v = b_w_in_d.rearrange("(kc p) n -> p kc n", p=128)

        def emit_b_weights(after_op):
            for qi in range(4):
                dma("pool", WB_in[:, :, qi * 512:(qi + 1) * 512], b_in_v[:, :, qi * 512:(qi + 1) * 512], "l_wbin%d" % qi,
                    writes=["WB_in%d" % qi], after=[after_op])
            dma("pool", WB_out, b_w_out_d.rearrange("(kc p) n -> p kc n", p=128), "l_wbout", writes=["WB_out"], after=[after_op])
            P.op("act", lambda e: e.activation(out=esink[:, :], in_=esink[:, :], func=AF.Exp), reads=["esink"], writes=["esink"])

        def emit_b_tables():
            dma("pool", biasT.rearrange("p j n -> p (j n)"), bias_d, "l_bias", writes=["yT%d" % j for j in range(8)] + ["biasT"])
            dma("sp", g_post[:, :], g_bpost_d, "l_gbpost", writes=["g_post"])

        out_ops = []
        colsB = {}

        def b_kv(blks, halo):
            NT = 128 * len(blks)
            s0 = 0 if halo else 1
            bi = bank()
            for kc in range(8):
                P.op("pe", lambda e, bi=bi, kc=kc: e.matmul(
                    BK(bi, NT), lhsT=wkp[:, kc, :], rhs=xT[:, kc, 2:2 + NT], start=(kc == 0), stop=(kc == 7)),
                    reads=["xT", "wkp"], writes=["ps%d" % bi])
            P.op("dve", lambda e, bi=bi: e.tensor_copy(out=kt_nat[:, 0:NT], in_=BK(bi, NT)), reads=["ps%d" % bi], writes=["kt_nat"])
            for kvh in range(2):
                for hh in range(2):
                    slot_keys = ["kt%d_%d%d" % (s0 + i, kvh, hh) for i in range(len(blks))]
                    dst = ktp[hh * 64:(hh + 1) * 64, kvh, hh, s0 * 128:s0 * 128 + NT]
                    src = kt_nat[kvh * 64:(kvh + 1) * 64, 0:NT]
                    if kvh == hh:
                        P.op("act", lambda e, dst=dst, src=src: e.copy(out=dst, in_=src), reads=["kt_nat"], writes=slot_keys)
                    else:
                        dma("sp", dst, src, "l_kt%d" % kvh, reads=["kt_nat"], writes=slot_keys)
            for i in range(len(blks)):
                bi = bank()
                for kc in range(8):
                    P.op("pe", lambda e, bi=bi, kc=kc, i=i: e.matmul(
                        BK(bi, 128), lhsT=xT[:, kc, 2 + i * 128:2 + (i + 1) * 128], rhs=wv[:, kc, :], start=(kc == 0), stop=(kc == 7)),
                        reads=["xT", "wv"], writes=["ps%d" % bi])
                for hh in range(2):
                    P.op("act", lambda e, bi=bi, i=i, hh=hh: e.copy(
                        out=vp[:, s0 + i, :, hh, hh * 64:(hh + 1) * 64], in_=BK(bi, 128).rearrange("p (a b) -> p a b", a=2)),
                        reads=["ps%d" % bi], writes=["vp%d" % (s0 + i)])

        def b_qz():
            for j in range(8):
                bi = bank()
                for kc in range(8):
                    P.op("pe", lambda e, bi=bi, kc=kc, j=j: e.matmul(
                        BK(bi), lhsT=WB_in[:, kc, j * 128:(j + 1) * 128], rhs=xT[:, kc, 2:514], start=(kc == 0), stop=(kc == 7)),
                        reads=["xT", "WB_in%d" % (j // 4)], writes=["ps%d" % bi])
                P.op("dve", lambda e, bi=bi, j=j: e.tensor_scalar(out=QT[:, j, :], in0=BK(bi), scalar1=0.125, scalar2=None, op0=ALU.mult),
                     reads=["ps%d" % bi], writes=["QT%d" % j])
            for j in range(8):
                bi = bank()
                for kc in range(8):
                    P.op("pe", lambda e, bi=bi, kc=kc, j=j: e.matmul(
                        BK(bi), lhsT=WB_in[:, kc, 1024 + j * 128:1024 + (j + 1) * 128], rhs=xT[:, kc, 2:514], start=(kc == 0), stop=(kc == 7)),
                        reads=["xT", "WB_in%d" % (2 + j // 4)], writes=["ps%d" % bi])
                P.op("act", lambda e, bi=bi, j=j: e.activation(out=szT[:, j, :], in_=BK(bi), func=AF.Silu),
                     reads=["ps%d" % bi], writes=["szT%d" % j])

        def b_scores(j, i, par, first_global):
            kvh = j // 4
            bi = bankB()
            P.op("pe", lambda e: e.matmul(BK(bi), lhsT=ident[:, :], rhs=biasT[:, j, :], start=True, stop=False),
                 reads=["ident", "biasT"], writes=["ps%d" % bi])
            for hh in range(2):
                for kb in range(2):
                    slot = i + kb
                    P.op("pe", lambda e, hh=hh, kb=kb, slot=slot: e.matmul(
                        BK(bi, 128, (hh * 2 + kb) * 128), lhsT=ktp[:, kvh, hh, slot * 128:(slot + 1) * 128],
                        rhs=QT[:, j, i * 128:(i + 1) * 128], start=False, stop=(hh == 1 and kb == 1)),
                        reads=["kt%d_%d%d" % (slot, kvh, hh), "QT%d" % j], writes=["ps%d" % bi])
            P.op("act", lambda e: e.activation(out=PT[par][:, j, :], in_=BK(bi), func=AF.Exp),
                 reads=["ps%d" % bi], writes=["PT%d_%d" % (par, j)])
            if first_global:
                view = PT[par][:, j, :].rearrange("p (hh kb q) -> p hh kb q", hh=2, kb=2)[:, :, 0, :]
                P.op("dve", lambda e: e.tensor_scalar(out=view, in0=view, scalar1=flag[:, 0:1], scalar2=None, op0=ALU.mult),
                     reads=["flag", "PT%d_%d" % (par, j)], writes=["PT%d_%d" % (par, j)])

        od_state = {}

        def b_pv(j, i, par):
            kvh = j // 4
            q = j % 2
            if q == 0:
                od_state["bank"] = bankB()
            bi = od_state["bank"]
            rp = (j // 2) % 2
            pv = PT[par][:, j, :].rearrange("p (hh kb q) -> p hh kb q", hh=2, kb=2)
            for which in range(2):
                n = 0
                for hh in range(2):
                    for kb in range(2):
                        slot = i + kb
                        lhs = vp[:, slot, kvh, hh, :] if which == 0 else onesp[:, hh, :]
                        P.op("pe", lambda e, lhs=lhs, hh=hh, kb=kb, which=which, n=n: e.matmul(
                            BK(bi, 128, which * 256 + q * 128), lhsT=lhs, rhs=pv[:, hh, kb, :], start=(n == 0), stop=(n == 3)),
                            reads=["vp%d" % slot, "onesp", "PT%d_%d" % (par, j)], writes=["ps%d" % bi])
                        n += 1
            if q == 0:
                return
            P.op("dve", lambda e: e.tensor_tensor(
                out=g1buf[:, rp, :].rearrange("p (a b) -> p a b", a=2), in0=BK(bi, 256, 0).rearrange("p (a b) -> p a b", a=2),
                in1=szT[:, j - 1:j + 1, i * 128:(i + 1) * 128], op=ALU.mult),
                reads=["ps%d" % bi, "szT%d" % (j - 1), "szT%d" % j], writes=["g1buf%d" % rp, "szb"])
            for qq in range(2):
                jj = j - 1 + qq
                P.op("dve", lambda e, qq=qq, jj=jj: e.tensor_scalar(
                    out=rbuf[:, rp, qq * 128:(qq + 1) * 128], in0=BK(bi, 128, 256 + qq * 128), scalar1=esink[:, jj:jj + 1],
                    scalar2=None, op0=ALU.add), reads=["ps%d" % bi, "esink"], writes=["rbuf%d" % rp, "t1"])
            P.op("act", lambda e: e.activation(out=rbuf[:, rp, :], in_=rbuf[:, rp, :], func=AF.Ln),
                 reads=["rbuf%d" % rp], writes=["rbuf%d" % rp])
            P.op("act", lambda e: e.activation(out=rbuf[:, rp, :], in_=rbuf[:, rp, :], func=AF.Exp, scale=-1.0),
                 reads=["rbuf%d" % rp], writes=["rbuf%d" % rp])
            P.op("pool", lambda e: e.tensor_tensor(
                out=gT[par][:, j - 1:j + 1, :], in0=g1buf[:, rp, :].rearrange("p (a b) -> p a b", a=2),
                in1=rbuf[:, rp, :].rearrange("p (a b) -> p a b", a=2), op=ALU.mult),
                reads=["g1buf%d" % rp, "rbuf%d" % rp],
                writes=["gT%d_%d" % (par, j - 1), "gT%d_%d" % (par, j), "c_sb", "vbuf"])

        def b_outproj(blk, par):
            pi = 2
            for kc in range(8):
                for half in range(2):
                    bi = 2 * pi + half
                    P.op("pe", lambda e, bi=bi, kc=kc, half=half: e.matmul(
                        BK(bi), lhsT=gT[par][:, kc, :], rhs=WB_out[:, kc, half * 512:(half + 1) * 512],
                        start=(kc == 0), stop=(kc == 7)), reads=["gT%d_%d" % (par, kc), "WB_out"], writes=["ps%d" % bi])
            post_norm_residual(pi, blk)
            ob = blk - 1
            out_ops.append(dma("sp", out_d[ob * 128:(ob + 1) * 128, :], h[:, blk, :], "s_out%d" % (ob // 4),
                               reads=["h%d" % blk]))

        orderA = [(groups[1], "first"), (groups[0], "halo")] + [(g_, "normal") for g_ in groups[2:]]
        cpre = pre_stats()
        pre_apply(cpre)
        set_left_cols(1)
        cols = {}
        cs_ = [norm_stats([blk]) for blk in orderA[0][0]]
        cols[0] = cs_[0]
        for i, blk in enumerate(orderA[0][0]):
            norm_apply(blk, cs_[i], i)
        kv_fold_prep()
        PAIR_SEQ = [2, 0, 1, 2]
        for gi, (blks, mode) in enumerate(orderA):
            nxt = orderA[gi + 1][0] if gi + 1 < len(orderA) else None
            nxt_mode = orderA[gi + 1][1] if gi + 1 < len(orderA) else None
            nT = len(nxt) if nxt is not None else 0

            def hook(j, nxt=nxt, gi=gi, nT=nT):
                if nxt is None:
                    firstB = [groups[0][0]] + list(groups[1])
                    if j == 0:
                        dma("pool", g_pre[:, :], g_bpre_d, "l_gbpre", writes=["g_pre"])
                        colsB["c0"] = newcols(5)
                    if 1 <= j <= 5:
                        blk_ = firstB[j - 1]
                        c_ = colsB["c0"] + j - 1
                        P.op("dve", lambda e: e.scalar_tensor_tensor(
                            out=xn2, in0=h[:, blk_, :], scalar=1.0, in1=h[:, blk_, :], op0=ALU.mult, op1=ALU.mult,
                            accum_out=ss[:, c_:c_ + 1]), reads=["h%d" % blk_, "ss"], writes=["junk", "ss%d" % c_])
                    if j == 5:
                        rstd_cols(colsB["c0"], 5, ["ss%d" % (colsB["c0"] + k_) for k_ in range(5)])
                        colsB[0] = colsB["c0"]
                        colsB[1] = colsB["c0"] + 1
                    if j == 6:
                        norm_apply_dve(0, colsB[0])
                    if j == 7:
                        norm_apply_dve(groups[1][0], colsB[1])
                    return
                if gi == 0:
                    kv_fold(j)
                if j == 2:
                    cols[gi + 1] = norm_stats(nxt)
                j0_ = 3 if gi == 1 else 5
                if j == j0_:
                    norm_apply_dve(nxt[0], cols[gi + 1])
                if j == j0_ + 1 and nT > 1:
                    norm_apply_dve(nxt[1], cols[gi + 1] + 1)

            a_inproj(blks, mode, hook)
            if nxt_mode == "halo":
                set_left_cols(0)
            t_done = [0]

            def do_T(nxt=nxt, gi=gi, nT=nT, t_done=t_done):
                k = t_done[0]
                norm_apply_pe(k)
                if k + 2 < nT:
                    norm_apply_dve(nxt[k + 2], cols[gi + 1] + k + 2)
                t_done[0] += 1

            if nxt is None:
                last_inproj = [o for o in P.ops if o.eng == "pe"][-1]
                emit_b_weights(last_inproj)
                phase[0] = "B"
                g1b = groups[1]
                norm_apply_pe(0)
                norm_apply_dve(g1b[1], colsB[1] + 1)
                a_outproj(blks[0], 0, PAIR_SEQ[0])
                bank_ctr[0] = 2
                b_kv(groups[0], True)
                norm_apply_pe(0)
                norm_apply_dve(g1b[2], colsB[1] + 2)
                norm_apply_pe(1)
                norm_apply_dve(g1b[3], colsB[1] + 3)
                a_outproj(blks[1], 1, PAIR_SEQ[1])
                norm_apply_pe(2)
                a_outproj(blks[2], 2, PAIR_SEQ[2])
                norm_apply_pe(3)
                a_outproj(blks[3], 3, PAIR_SEQ[3])
                emit_b_tables()
                continue
            for _ in range(min(2, nT)):
                do_T()
            for i in range(len(blks)):
                a_outproj(blks[i], i, PAIR_SEQ[i % 4] if len(blks) > 1 else 2)
                if t_done[0] < nT:
                    do_T()
            while t_done[0] < nT:
                do_T()

        if debug:
            dbg_ops = [dma("sp", dbg_d.rearrange("(b p) n -> p b n", p=128), h[:, :, :], "s_dbg",
                           reads=["h%d" % b for b in range(NB1)])]
        else:
            dbg_ops = []

        LAG = 2
        for gi in range(1, len(groups)):
            blks = groups[gi]
            nxt = groups[gi + 1] if gi + 1 < len(groups) else None
            bank_ctr[0] = 0
            b_kv(blks, False)
            b_qz()
            if nxt is not None:
                colsB[gi + 1] = norm_stats(nxt)
            items = [(i, j) for i in range(len(blks)) for j in range(8)]
            nI = len(items)
            OP_DELAY = 2
            for t in range(nI + LAG + OP_DELAY):
                if t < nI:
                    i, j = items[t]
                    b_scores(j, i, i % 2, blks[i] == 1)
                    if nxt is not None and j == 0:
                        norm_apply_dve(nxt[i], colsB[gi + 1] + i)
                    if nxt is not None and j == 6:
                        norm_apply_pe(i)
                if LAG <= t < nI + LAG:
                    i, j = items[t - LAG]
                    b_pv(j, i, i % 2)
                    if t - LAG == nI - 1:
                        P.op("dve", lambda e: e.tensor_copy(out=ktp[:, :, :, 0:128], in_=ktp[:, :, :, 512:640]),
                             reads=KT_ALL(4), writes=KT_ALL(0))
                        P.op("dve", lambda e: e.tensor_copy(out=vp[:, 0, :, :, :], in_=vp[:, 4, :, :, :]), reads=["vp4"], writes=["vp0"])
                if LAG + OP_DELAY <= t:
                    i, j = items[t - LAG - OP_DELAY]
                    if j == 7:
                        b_outproj(blks[i], i % 2)

        finals = [out_ops[4 * g + 3] for g in range(NGRP)] + dbg_ops
        for o in out_ops:
            o.signal = True
        P.emit(final_waits=finals)
    nc._prog = P
    return nc


def _t5_bucket(dist):
    n_buckets, max_distance = 32, 128
    max_exact = n_buckets // 2
    d = np.maximum(dist, 1).astype(np.float32)
    large = max_exact + (np.log(d / np.float32(max_exact)) / np.float32(math.log(max_distance / max_exact))
                         * np.float32(n_buckets - max_exact)).astype(np.int32)
    large = np.minimum(large, n_buckets - 1)
    return np.where(dist < max_exact, dist, large)


def _bias_table(rel_bias):
    s = np.arange(128)[:, None]
    q = np.arange(128)[None, :]
    tab = np.empty((128, 8, 2, 2, 128), np.float32)
    rb = np.asarray(rel_bias, np.float32)
    for kb in range(2):
        dist = q - s + (128 if kb == 0 else 0)
        valid = (dist >= 0) & (dist < 128)
        bucket = _t5_bucket(np.clip(dist, 0, None).astype(np.int32))
        g = rb[bucket]
        g = np.where(valid[:, :, None], g, np.float32(MASK_VAL))
        tab[:, :, :, kb, :] = g.transpose(0, 2, 1).reshape(128, 8, 2, 128)
    return np.ascontiguousarray(tab.reshape(128, 4096))


_NC_CACHE = {}


def _host_inputs(x, a_pre_norm, a_w_in, a_conv_w, a_w_out, a_post_norm, kv_norm, w_kv, rel_bias,
                 b_pre_norm, b_w_in, b_sinks, b_w_out, b_post_norm):
    f = lambda a: np.ascontiguousarray(np.asarray(a, np.float32))
    x = f(x)
    bc = lambda v: np.ascontiguousarray(np.broadcast_to(f(v).reshape(1, D), (128, D)))
    percol = lambda v: np.ascontiguousarray(f(v).reshape(8, 128).T)
    cwl = np.ascontiguousarray(f(a_conv_w).reshape(3, 8, 128).transpose(2, 1, 0).reshape(128, 24))
    sk = f(b_sinks).reshape(8, 2)
    sinks_l = np.ascontiguousarray(np.repeat(sk.T, 64, axis=0))
    onesp = np.zeros((128, 2, 128), np.float32)
    onesp[:, 0, 0:64] = 1.0
    onesp[:, 1, 64:128] = 1.0
    shared = {
        "ident": np.eye(128, dtype=np.float32),
        "onesp": onesp.reshape(128, 256),
        "a_w_in": f(a_w_in).reshape(D, 4 * D),
        "a_w_out": f(a_w_out).reshape(D, D),
        "b_w_in": f(b_w_in).reshape(D, 2 * D),
        "b_w_out": f(b_w_out).reshape(D, D),
        "w_kv": f(w_kv),
        "g_apre": bc(a_pre_norm), "g_apost": bc(a_post_norm), "g_bpre": bc(b_pre_norm), "g_bpost": bc(b_post_norm),
        "gk": percol(kv_norm), "gb": percol(b_pre_norm),
        "cw": cwl, "sinks": sinks_l, "biasT": _bias_table(rel_bias),
    }
    in_maps = []
    for c in range(NCORES):
        b, qd = divmod(c, 4)
        start = qd * TOK_PER_CORE
        xs = np.zeros((130 + TOK_PER_CORE, D), np.float32)
        lo = start - 130
        if lo >= 0:
            xs[:] = x[b, lo:start + TOK_PER_CORE]
        else:
            xs[130:] = x[b, 0:TOK_PER_CORE]
        m = dict(shared)
        m["xpre"] = np.ascontiguousarray(np.concatenate([xs[0:2], xs[128:130]], axis=0))
        m["xmain"] = np.ascontiguousarray(xs[2:])
        m["flag"] = np.full((128, 1), 0.0 if qd == 0 else 1.0, np.float32)
        in_maps.append(m)
    return in_maps


def kernel(**inputs):
    in_maps = _host_inputs(**inputs)
    if "nc" not in _NC_CACHE:
        _NC_CACHE["nc"] = build_nc()
    nc = _NC_CACHE["nc"]
    res = run_bass_kernel_spmd(nc, in_maps, core_ids=list(range(NCORES)))
    out = np.empty((2, 4 * TOK_PER_CORE, D), np.float32)
    for c in range(NCORES):
        b, qd = divmod(c, 4)
        out[b, qd * TOK_PER_CORE:(qd + 1) * TOK_PER_CORE] = res.results[c]["out"]
    return out
```
